# Optimizing a Trainium2 kernel written in Bass

```python
import jax, jax.numpy as jnp
from jax import lax
import numpy as np

D_MODEL = 2048
BATCH = 4
SEQ = 2048
DEPTH = 1
DEC_BATCH = 16
DEC_SEQ = 64
PAST_LEN = 2048

CHUNK = 64
GMLP_CHUNK = 128
GMLP_WIDTH = D_MODEL
GMLP_GROUPS = 8
GMLP_GROUP_DIM = GMLP_WIDTH // GMLP_GROUPS
N_HEADS = 32
N_KV_HEADS = 4
HEAD_DIM = 64
Q_REP = N_HEADS // N_KV_HEADS
WINDOW = 128
WIN_CHUNKS = WINDOW // CHUNK
ROPE_THETA = 500000.0
ROT_DIM = HEAD_DIM // 4
D_FF = ((8 * D_MODEL // 3 + 255) // 256) * 256
Q_WIDTH = N_HEADS * HEAD_DIM
KV_WIDTH = N_KV_HEADS * HEAD_DIM
IN_COLS = 2 * GMLP_WIDTH + Q_WIDTH + 2 * KV_WIDTH
EPS = 1e-6
NEG = -1e30

kernel_name = "hybrid_gmlp_swa_streaming_step"


def _rmsnorm(x, g):
    xf = x.astype(jnp.float32)
    y = xf * lax.rsqrt(jnp.mean(xf * xf, axis=-1, keepdims=True) + EPS)
    return (y * g.astype(jnp.float32)).astype(x.dtype)


def _layernorm(x, g, b):
    xf = x.astype(jnp.float32)
    mu = jnp.mean(xf, axis=-1, keepdims=True)
    var = jnp.mean(jnp.square(xf - mu), axis=-1, keepdims=True)
    y = (xf - mu) * lax.rsqrt(var + EPS)
    return (y * g.astype(jnp.float32) + b.astype(jnp.float32)).astype(x.dtype)


def _partial_rope(x, pos):
    half = ROT_DIM // 2
    inv_freq = jnp.float32(ROPE_THETA) ** (-(jnp.arange(half, dtype=jnp.float32) * 2.0 / ROT_DIM))
    ang = pos[:, None] * inv_freq[None, :]
    cos = jnp.cos(ang)[:, None, :]
    sin = jnp.sin(ang)[:, None, :]
    xr = x[..., :ROT_DIM].astype(jnp.float32)
    x1, x2 = xr[..., :half], xr[..., half:]
    rot = jnp.concatenate([x1 * cos - x2 * sin, x2 * cos + x1 * sin], axis=-1)
    return jnp.concatenate([rot.astype(x.dtype), x[..., ROT_DIM:]], axis=-1)


def _project(x, norm_g, w_in):
    B, S, _ = x.shape
    xn = _rmsnorm(x, norm_g)
    h = xn @ w_in
    z = jax.nn.gelu(h[..., :2 * GMLP_WIDTH])
    u, v = z[..., :GMLP_WIDTH], z[..., GMLP_WIDTH:]
    o = 2 * GMLP_WIDTH
    q = h[..., o:o + Q_WIDTH].reshape(B, S, N_HEADS, HEAD_DIM)
    k = h[..., o + Q_WIDTH:o + Q_WIDTH + KV_WIDTH].reshape(B, S, N_KV_HEADS, HEAD_DIM)
    va = h[..., o + Q_WIDTH + KV_WIDTH:].reshape(B, S, N_KV_HEADS, HEAD_DIM)
    return xn, u, v, q, k, va


def _gmlp_mix(u, vn, ws, bs):
    B, N, L, _ = vn.shape
    i = jnp.arange(GMLP_CHUNK)
    mask = (i[None, :] // CHUNK) <= (i[:, None] // CHUNK)
    wm = (ws * mask.astype(ws.dtype))[:, :L, :L]
    vg = vn.reshape(B, N, L, GMLP_GROUPS, GMLP_GROUP_DIM)
    s = jnp.einsum('gij,bnjgc->bnigc', wm, vg) + bs[:, :L].T[None, None, :, :, None]
    return u * s.reshape(B, N, L, GMLP_WIDTH)


def _sink_softmax(s, sink):
    sk = sink.astype(jnp.float32)[:, :, None]
    m = jnp.maximum(jnp.max(s, axis=-1), sk)
    p = jnp.exp(s - m[..., None])
    den = jnp.sum(p, axis=-1) + jnp.exp(sk - m)
    return p / den[..., None]


def _swa_prompt(q, k, v, sinks):
    B, S, _, _ = q.shape
    nC = S // CHUNK
    padw = ((0, 0), (WINDOW, 0), (0, 0), (0, 0))
    kp = jnp.pad(k, padw).reshape(B, nC + WIN_CHUNKS, CHUNK, N_KV_HEADS, HEAD_DIM)
    vp = jnp.pad(v, padw).reshape(B, nC + WIN_CHUNKS, CHUNK, N_KV_HEADS, HEAD_DIM)
    kb = jnp.concatenate([kp[:, i:i + nC] for i in range(WIN_CHUNKS + 1)], axis=2)
    vb = jnp.concatenate([vp[:, i:i + nC] for i in range(WIN_CHUNKS + 1)], axis=2)
    key_chunk = jnp.arange(nC)[:, None] + jnp.repeat(jnp.arange(WIN_CHUNKS + 1), CHUNK)[None, :] - WIN_CHUNKS
    valid = key_chunk >= 0
    qg = q.reshape(B, nC, CHUNK, N_KV_HEADS, Q_REP, HEAD_DIM)
    s = jnp.einsum('bnqhrd,bnkhd->bnhrqk', qg, kb).astype(jnp.float32) * (HEAD_DIM ** -0.5)
    s = jnp.where(valid[None, :, None, None, None, :], s, NEG)
    p = _sink_softmax(s, sinks.reshape(N_KV_HEADS, Q_REP)).astype(v.dtype)
    o = jnp.einsum('bnhrqk,bnkhd->bnqhrd', p, vb)
    return o.reshape(B, S, Q_WIDTH)


def _swa_sample(q, k_all, v_all, sinks):
    B, T, _, _ = q.shape
    qg = q.reshape(B, T, N_KV_HEADS, Q_REP, HEAD_DIM)
    s = jnp.einsum('bqhrd,bkhd->bhrqk', qg, k_all).astype(jnp.float32) * (HEAD_DIM ** -0.5)
    p = _sink_softmax(s, sinks.reshape(N_KV_HEADS, Q_REP)).astype(v_all.dtype)
    o = jnp.einsum('bhrqk,bkhd->bqhrd', p, v_all)
    return o.reshape(B, T, Q_WIDTH)


def _merge_and_ffn(x, xn, a, o, w_gate, b_gate, w_branch_a, w_branch_b, w_out,
                   norm_ffn_g, w_ffn_gate, w_ffn_up, w_ffn_down):
    g = jax.nn.sigmoid(xn @ w_gate + b_gate)
    g_a, g_b = g[..., :D_MODEL], g[..., D_MODEL:]
    x = x + (g_a * (a @ w_branch_a) + g_b * (o @ w_branch_b)) @ w_out
    h = _rmsnorm(x, norm_ffn_g)
    return x + (jax.nn.silu(h @ w_ffn_gate) * (h @ w_ffn_up)) @ w_ffn_down


def setup_inputs(seed: int = 0) -> dict:
    key = jax.random.key(seed)
    ks = jax.random.split(key, 24)
    f32 = jnp.float32
    swa_len = min(WINDOW, PAST_LEN)

    def nrm(k, shape, scale):
        return jax.random.normal(k, shape, f32) * scale

    return {
        "x_prompt": nrm(ks[0], (BATCH, SEQ, D_MODEL), 1.0),
        "x_sample": nrm(ks[1], (DEC_BATCH, DEC_SEQ, D_MODEL), 1.0),
        "cache_swa_k": nrm(ks[2], (DEPTH, DEC_BATCH, swa_len, N_KV_HEADS, HEAD_DIM), 1.0),
        "cache_swa_v": nrm(ks[3], (DEPTH, DEC_BATCH, swa_len, N_KV_HEADS, HEAD_DIM), 1.0),
        "norm_mix_g": 1.0 + nrm(ks[4], (DEPTH, D_MODEL), 0.02),
        "w_in": nrm(ks[5], (DEPTH, D_MODEL, IN_COLS), D_MODEL ** -0.5),
        "gmlp_ln_g": 1.0 + nrm(ks[6], (DEPTH, GMLP_WIDTH), 0.02),
        "gmlp_ln_b": nrm(ks[7], (DEPTH, GMLP_WIDTH), 0.02),
        "gmlp_ws": nrm(ks[8], (DEPTH, GMLP_GROUPS, GMLP_CHUNK, GMLP_CHUNK), GMLP_CHUNK ** -0.5),
        "gmlp_bs": 1.0 + nrm(ks[9], (DEPTH, GMLP_GROUPS, GMLP_CHUNK), 0.1),
        "attn_sinks": nrm(ks[10], (DEPTH, N_HEADS), 0.5),
        "w_gate": nrm(ks[11], (DEPTH, D_MODEL, 2 * D_MODEL), D_MODEL ** -0.5),
        "b_gate": nrm(ks[12], (DEPTH, 2 * D_MODEL), 0.02),
        "w_branch_a": nrm(ks[13], (DEPTH, GMLP_WIDTH, D_MODEL), GMLP_WIDTH ** -0.5),
        "w_branch_b": nrm(ks[14], (DEPTH, Q_WIDTH, D_MODEL), Q_WIDTH ** -0.5),
        "w_out": nrm(ks[15], (DEPTH, D_MODEL, D_MODEL), D_MODEL ** -0.5),
        "norm_ffn_g": 1.0 + nrm(ks[16], (DEPTH, D_MODEL), 0.02),
        "w_ffn_gate": nrm(ks[17], (DEPTH, D_MODEL, D_FF), D_MODEL ** -0.5),
        "w_ffn_up": nrm(ks[18], (DEPTH, D_MODEL, D_FF), D_MODEL ** -0.5),
        "w_ffn_down": nrm(ks[19], (DEPTH, D_FF, D_MODEL), D_FF ** -0.5),
        "final_norm_g": 1.0 + nrm(ks[20], (D_MODEL,), 0.02),
    }


def reference(x_prompt, x_sample, cache_swa_k, cache_swa_v, norm_mix_g, w_in, gmlp_ln_g, gmlp_ln_b,
              gmlp_ws, gmlp_bs, attn_sinks, w_gate, b_gate, w_branch_a, w_branch_b, w_out,
              norm_ffn_g, w_ffn_gate, w_ffn_up, w_ffn_down, final_norm_g):
    Bp, S, _ = x_prompt.shape
    Bs, T, _ = x_sample.shape
    pos_p = jnp.arange(S, dtype=jnp.float32)
    pos_s = PAST_LEN + jnp.arange(T, dtype=jnp.float32)
    keep_p = min(WINDOW, S)
    xp, xs = x_prompt, x_sample
    kp_list, vp_list, ks_list, vs_list, gv_list = [], [], [], [], []
    for l in range(DEPTH):
        xn, u, v, q, k, va = _project(xp, norm_mix_g[l], w_in[l])
        q, k = _partial_rope(q, pos_p), _partial_rope(k, pos_p)
        vn = _layernorm(v, gmlp_ln_g[l], gmlp_ln_b[l])
        nb = S // GMLP_CHUNK
        a = _gmlp_mix(u.reshape(Bp, nb, GMLP_CHUNK, GMLP_WIDTH),
                      vn.reshape(Bp, nb, GMLP_CHUNK, GMLP_WIDTH), gmlp_ws[l], gmlp_bs[l]).reshape(Bp, S, GMLP_WIDTH)
        o = _swa_prompt(q, k, va, attn_sinks[l])
        xp = _merge_and_ffn(xp, xn, a, o, w_gate[l], b_gate[l], w_branch_a[l], w_branch_b[l], w_out[l],
                            norm_ffn_g[l], w_ffn_gate[l], w_ffn_up[l], w_ffn_down[l])
        kp_list.append(k[:, S - keep_p:])
        vp_list.append(va[:, S - keep_p:])
        xn_s, u_s, v_s, q_s, k_s, va_s = _project(xs, norm_mix_g[l], w_in[l])
        q_s, k_s = _partial_rope(q_s, pos_s), _partial_rope(k_s, pos_s)
        vn_s = _layernorm(v_s, gmlp_ln_g[l], gmlp_ln_b[l])
        a_s = _gmlp_mix(u_s[:, None], vn_s[:, None], gmlp_ws[l], gmlp_bs[l])[:, 0]
        k_all = jnp.concatenate([cache_swa_k[l].astype(k_s.dtype), k_s], axis=1)
        v_all = jnp.concatenate([cache_swa_v[l].astype(va_s.dtype), va_s], axis=1)
        o_s = _swa_sample(q_s, k_all, v_all, attn_sinks[l])
        xs = _merge_and_ffn(xs, xn_s, a_s, o_s, w_gate[l], b_gate[l], w_branch_a[l], w_branch_b[l], w_out[l],
                            norm_ffn_g[l], w_ffn_gate[l], w_ffn_up[l], w_ffn_down[l])
        ks_list.append(k_s)
        vs_list.append(va_s)
        gv_list.append(vn_s)
    y_prompt = _rmsnorm(xp, final_norm_g)
    y_sample = _rmsnorm(xs, final_norm_g)
    swa_k_prompt = jnp.stack(kp_list, axis=0)
    swa_v_prompt = jnp.stack(vp_list, axis=0)
    swa_k_sample = jnp.stack(ks_list, axis=0)
    swa_v_sample = jnp.stack(vs_list, axis=0)
    gmlp_v_sample = jnp.stack(gv_list, axis=0)
    return (y_prompt, y_sample, swa_k_prompt, swa_v_prompt, swa_k_sample, swa_v_sample, gmlp_v_sample)
```

```python
import numpy as np
from contextlib import ExitStack
import concourse.bass as bass
import concourse.mybir as mybir
from concourse.bass_utils import run_bass_kernel_spmd

F32 = mybir.dt.float32
BF16 = mybir.dt.bfloat16
AF = mybir.ActivationFunctionType
ALU = mybir.AluOpType

D = 2048
DFF = 5632
T = 384
NT = 3
NPASS = 3
NTOK = 1152
EPS = 1e-6
NPARTS = 134
PART_SPECS = []


class Op:
    __slots__ = ("eng", "fn", "deps", "is_dma", "chan", "chan_cnt", "sig", "needed", "idx")


class Prog:
    def __init__(self):
        self.ops = []
        self.lastw = {}
        self.readers = {}
        self.fence = {}
        self.chan_cnt = {}

    @staticmethod
    def _compress(deps):
        best = {}
        for d in deps:
            k = ("c", d.chan) if d.is_dma else ("e", d.eng)
            o = best.get(k)
            if o is None or d.idx > o.idx:
                best[k] = d
        return list(best.values())

    def add(self, eng, fn, reads=(), writes=(), chan=None):
        op = Op()
        op.eng, op.fn, op.is_dma, op.chan = eng, fn, chan is not None, chan
        op.sig, op.needed, op.chan_cnt = 0, False, 0
        writes = list(writes) + [k for k in reads if k[0] in ("ps", "pst")]
        reads = [k for k in reads if k[0] not in ("ps", "pst")]
        deps = set()
        for k in reads:
            w = self.lastw.get(k)
            if w is not None:
                deps.add(w)
            elif k[0] in self.fence:
                deps.update(self.fence[k[0]])
        for k in writes:
            w = self.lastw.get(k)
            if w is not None:
                deps.add(w)
            elif k[0] in self.fence:
                deps.update(self.fence[k[0]])
            rs = self.readers.get(k)
            if rs:
                deps.update(rs)
        op.deps = self._compress(deps)
        if chan is not None:
            self.chan_cnt[chan] = self.chan_cnt.get(chan, 0) + 16
            op.chan_cnt = self.chan_cnt[chan]
        op.idx = len(self.ops)
        self.ops.append(op)
        for k in reads:
            self.readers.setdefault(k, []).append(op)
        for k in writes:
            self.lastw[k] = op
            self.readers[k] = []
        return op

    def relayout(self, region):
        s = set(self.fence.get(region, ()))
        for k in list(self.lastw):
            if k[0] == region:
                s.add(self.lastw.pop(k))
        for k in list(self.readers):
            if k[0] == region:
                s.update(self.readers.pop(k))
        self.fence[region] = self._compress(s)

    def emit(self, nc, es):
        for op in self.ops:
            for d in op.deps:
                if not d.is_dma:
                    d.needed = True
        cnt = {}
        for op in self.ops:
            if not op.is_dma and op.needed:
                cnt[op.eng] = cnt.get(op.eng, 0) + 1
                op.sig = cnt[op.eng]
        esem = {e: es.enter_context(nc.semaphore("s_" + e)) for e in ("pe", "act", "dve", "pool", "sp")}
        csem = {c: es.enter_context(nc.semaphore("c_" + c)) for c in self.chan_cnt}
        import os as _os
        lim = int(_os.environ.get("KLIMIT", "0"))
        ops = self.ops[:lim] if lim > 0 else self.ops
        chan_cnt = {}
        for op in ops:
            if op.is_dma:
                chan_cnt[op.chan] = op.chan_cnt

        def run(e, engobj):
            waited = {}
            for op in ops:
                if op.eng != e:
                    continue
                for d in op.deps:
                    if d.is_dma:
                        sem, val, key = csem[d.chan], d.chan_cnt, "c_" + d.chan
                    else:
                        if d.eng == "pe" and e == "pe" and not op.is_dma:
                            continue
                        sem, val, key = esem[d.eng], d.sig, "e_" + d.eng
                    if waited.get(key, 0) < val:
                        engobj.wait_ge(sem, val)
                        waited[key] = val
                ins = op.fn(engobj)
                if op.is_dma:
                    ins.then_inc(csem[op.chan], 16)
                elif op.needed:
                    ins.then_inc(esem[e], 1)
            if e == "sp":
                for c, v in chan_cnt.items():
                    engobj.wait_ge(csem[c], v)

        with nc.Block() as block:
            @block.tensor
            def _(t):
                run("pe", t)

            @block.scalar
            def _(a):
                run("act", a)

            @block.vector
            def _(v):
                run("dve", v)

            @block.gpsimd
            def _(g):
                run("pool", g)

            @block.sync
            def _(s):
                run("sp", s)


def build_nc():
    nc = bass.Bass("TRN2", target_bir_lowering=False)
    P = Prog()

    def din(name, shape):
        return nc.dram_tensor(name, list(shape), F32, kind="ExternalInput").ap()

    def dout(name, shape):
        return nc.dram_tensor(name, list(shape), F32, kind="ExternalOutput").ap()

    xtok = din("xtok", [NTOK, D])
    xhalo = din("xhalo", [128, D])
    cs_d = din("cs", [128 + NTOK, 16])
    hv_d = din("hv", [128, 64])
    ck_d = din("ck", [256, 256])
    cv_d = din("cv", [256, 256])
    wpk = din("wpk", [NPARTS, 128, 4096])
    w_in, w_gate, w_a, w_b, w_o, w_fg, w_fu, w_fd = "w_in", "w_gate", "w_a", "w_b", "w_o", "w_fg", "w_fu", "w_fd"
    del PART_SPECS[:]
    gmixT_d = din("gmixT", [128, 16])
    gffnT_d = din("gffnT", [128, 16])
    bgT_d = din("bgT", [128, 32])
    lng_d = din("lng", [1, D])
    lnb_d = din("lnb", [1, D])
    fing_d = din("fing", [1, D])
    wsT_d = din("wsT", [128, 8, 128])
    bsP_d = din("bsP", [1, 1024])
    bsS_d = din("bsS", [1, 1024])
    sinkE_d = din("sinkE", [4, 512])
    sel_d = din("sel", [4, 256])
    ident_d = din("ident", [128, 128])

    wsc = nc.dram_tensor("wsc", [NPARTS, 128, 4096], BF16).ap()

    y_o = dout("y", [NTOK, D])
    kp_o = dout("kp", [128, 256])
    vp_o = dout("vp", [128, 256])
    ks_o = dout("ks", [128, 256])
    vs_o = dout("vs", [128, 256])
    gv_o = dout("gv", [128, D])

    with ExitStack() as es:
        def sb(name, shape, dt):
            return es.enter_context(nc.sbuf_tensor("sb_" + name, list(shape), dt))

        wslot = [sb("wslot%d" % i, [128, 8, 512], BF16) for i in range(4)]
        bcA = sb("bcA", [128, D], F32)
        bcB = sb("bcB", [128, D], F32)
        xin = [sb("xin0", [128, D], F32), sb("xin1", [128, D], F32)]
        R1 = sb("R1", [128, 44 * T], BF16)
        R2 = sb("R2", [128, 16, T], BF16)
        R3 = sb("R3", [128, 16, T], BF16)
        R4 = sb("R4", [128, 16 * T], BF16)
        R56 = sb("R56", [128, 3 * D], F32)
        SC = sb("SC", [128, 3072], BF16)
        cs_all = sb("cs_all", [128, 10, 16], F32)
        q_tm = [sb("qtm%d" % i, [128, 512], BF16) for i in range(3)]
        k32 = sb("k32", [128, 4, 64], F32)
        va32 = sb("va32", [128, 256], F32)
        kbf2s = [sb("kbf2_%d" % i, [128, 4, 2, 64], BF16) for i in range(3)]
        KT2 = sb("KT2", [128, 4, 6 * 128], BF16)
        Vb = sb("Vb", [128, 6, 256], BF16)
        rt1 = sb("rt1", [128, 8, 8], F32)
        rt2 = sb("rt2", [128, 8, 8], F32)
        ident = sb("identb", [128, 128], BF16)
        wmP = sb("wmP", [128, 8, 128], BF16)
        wmS = sb("wmS", [128, 8, 128], BF16)
        crow = sb("crow", [128, 1024], F32)
        ones32 = sb("ones32", [128, 128], F32)
        esT = sb("esT", [4, 512], F32)
        sel = sb("sel", [4, 256], F32)
        ones_bf = sb("ones_bf", [128, 128], BF16)
        esHL = sb("esHL", [8, 512], BF16)
        sel8 = sb("sel8", [8, 256], BF16)
        bsHL = sb("bsHL", [128, 1024], BF16)
        lo_bf = sb("lo_bf", [128, 1024], BF16)
        onesH_bf = sb("onesH_bf", [128, 64], BF16)
        hv_sb = sb("hv_sb", [128, 64], F32)
        gmixT = sb("gmixT", [128, 16], F32)
        gffnT = sb("gffnT", [128, 16], F32)
        bgT = sb("bgT", [128, 32], F32)
        stat = sb("stat", [128, 64], F32)

        ps = es.enter_context(nc.psum_tensor("psm", [128, 6, 512], F32))
        pst = es.enter_context(nc.psum_tensor("pstr", [128, 2, 1024], BF16))

        v_g = R1[:, 0:2 * NT * D].bitcast(F32).rearrange("p (s d) -> p s d", s=NT)
        hid = R1[:, :].rearrange("p (c t) -> p c t", c=44)
        xnT = R2
        uT = R3
        xnTh = R3[:, :, 0:128]
        vn_bf = R4[:, :].rearrange("p (s d) -> p s d", s=NT)
        mT = R4[:, :].rearrange("p (c t) -> p c t", c=16)
        r56b = R56[:, :].bitcast(BF16)
        QT2 = r56b[:, 0:16 * T].rearrange("p (c t) -> p c t", c=16)
        oT = r56b[:, 16 * T:32 * T].rearrange("p (c t) -> p c t", c=16)
        gA = r56b[:, 0:4 * T].rearrange("p (c t) -> p c t", c=4)
        gB = r56b[:, 4 * T:8 * T].rearrange("p (c t) -> p c t", c=4)
        tA = R56[:, 4 * T:8 * T].rearrange("p (c t) -> p c t", c=4)
        x1 = R56[:, :].rearrange("p (s d) -> p s d", s=NT)
        xn_tm = SC[:, 0:D]
        PT = [SC[:, 0:1024].rearrange("p (a k n) -> p a k n", a=2, k=2),
              SC[:, 1024:2048].rearrange("p (a k n) -> p a k n", a=2, k=2)]
        rden = [SC[:, 2048:2560].bitcast(F32), SC[:, 2560:3072].bitcast(F32)]
        t2 = [SC[:, 0:768].bitcast(F32), SC[:, 768:1536].bitcast(F32)]

        for k_ in range(4):
            wslot.append(R1[:, k_ * 4096:(k_ + 1) * 4096].rearrange("p (k n) -> p k n", k=8))

        bank_ctr = [0]

        def nb():
            b = bank_ctr[0] % 6
            bank_ctr[0] += 1
            return b

        tr_ctr = [0]

        def ntb():
            b = tr_ctr[0] % 2
            tr_ctr[0] += 1
            return b

        def bcast_rows(dram_ap, n):
            return bass.AP(dram_ap.tensor, dram_ap.offset, [[0, 128], [1, n]])

        def dma(q, out, in_, reads, writes, chan):
            return P.add(q, lambda e, o=out, i=in_: e.dma_start(out=o, in_=i), reads=reads, writes=writes, chan=chan)

        dma("sp", xin[0][:, 0:128], ident_d[:, :], [], [("xin", 0)], "xin0")
        P.add("dve", lambda e: e.tensor_copy(out=ident[:], in_=xin[0][:, 0:128]), reads=[("xin", 0)], writes=[("ident",)])
        dma("sp", xin[1][:, 0:1024].rearrange("p (g i) -> p g i", g=8), wsT_d[:, :, :], [], [("xin", 1)], "xin1")
        P.add("dve", lambda e: e.tensor_copy(out=wmP[:], in_=xin[1][:, 0:1024].rearrange("p (g i) -> p g i", g=8)),
              reads=[("xin", 1)], writes=[("wmP",)])
        P.add("dve", lambda e: e.memset(wmP[64:128, :, 0:64], 0.0), reads=[], writes=[("wmP",)])
        P.add("dve", lambda e: e.memset(xin[0][:, 1024:2048], 0.0), reads=[], writes=[("xin", 0)])
        if True:
            dma("sp", xin[0][0:64, 1024:2048].rearrange("p (g i) -> p g i", g=8)[:, :, 0:64], wsT_d[0:64, :, 0:64],
                [], [("xin", 0)], "c_ws1")
            dma("sp", xin[0][64:128, 1024:2048].rearrange("p (g i) -> p g i", g=8)[:, :, 64:128], wsT_d[0:64, :, 0:64],
                [], [("xin", 0)], "c_ws2")
        P.add("dve", lambda e: e.tensor_copy(out=wmS[:], in_=xin[0][:, 1024:2048].rearrange("p (g i) -> p g i", g=8)),
              reads=[("xin", 0)], writes=[("wmS",)])
        dma("sp", crow[0:1, :], bsP_d[:, :], [], [("crow",)], "c_bsP")
        dma("sp", crow[32:33, :], bsS_d[:, :], [], [("crow",)], "c_bsS")
        P.add("dve", lambda e: e.memset(ones32[:], 1.0), writes=[("ones32",)])
        P.add("dve", lambda e: e.memset(ones_bf[:], 1.0), writes=[("ones_bf",)])
        dma("sp", hv_sb[:], hv_d[:, :], [], [("hv",)], "c_hv")
        P.add("dve", lambda e: e.tensor_copy(out=onesH_bf[:], in_=hv_sb[:]), reads=[("hv",)], writes=[("onesH",)])
        dma("sp", esT[:], sinkE_d[:, :], [], [("esT",)], "c_es")
        P.add("act", lambda e: e.activation(out=esT[:], in_=esT[:], func=AF.Exp), reads=[("esT",)], writes=[("esT",)])
        P.add("dve", lambda e: e.tensor_copy(out=esHL[0:4, :], in_=esT[:]), reads=[("esT",)], writes=[("esHL",)])
        P.add("dve", lambda e: e.tensor_copy(out=xin[1][0:4, 0:512], in_=esHL[0:4, :]), reads=[("esHL",)], writes=[("xin", 1)])
        P.add("dve", lambda e: e.tensor_tensor(out=xin[1][0:4, 512:1024], in0=esT[:], in1=xin[1][0:4, 0:512], op=ALU.subtract),
              reads=[("esT",), ("xin", 1)], writes=[("xin", 1)])
        P.add("dve", lambda e: e.tensor_copy(out=lo_bf[0:4, 0:512], in_=xin[1][0:4, 512:1024]), reads=[("xin", 1)], writes=[("lo_bf",)])
        dma("sp", esHL[4:8, :], lo_bf[0:4, 0:512], [("lo_bf",)], [("esHL",)], "c_eslo")
        dma("sp", xin[1][0:4, 1024:1280], sel_d[:, :], [], [("xin", 1)], "xin1")
        dma("sp", xin[1][4:8, 1024:1280], sel_d[:, :], [], [("xin", 1)], "xin1")
        P.add("dve", lambda e: e.tensor_copy(out=sel8[:], in_=xin[1][0:8, 1024:1280]), reads=[("xin", 1)], writes=[("sel8",)])
        for r in (0, 32):
            P.add("dve", lambda e, r=r: e.tensor_copy(out=bsHL[r:r + 1, :], in_=crow[r:r + 1, :]), reads=[("crow",)], writes=[("bsHL",)])
            P.add("dve", lambda e, r=r: e.tensor_copy(out=xin[1][r:r + 1, 0:1024], in_=bsHL[r:r + 1, :]), reads=[("bsHL",)], writes=[("xin", 1)])
            P.add("dve", lambda e, r=r: e.tensor_tensor(out=xin[1][r:r + 1, 1024:2048], in0=crow[r:r + 1, :], in1=xin[1][r:r + 1, 0:1024],
                                                         op=ALU.subtract), reads=[("crow",), ("xin", 1)], writes=[("xin", 1)])
            P.add("dve", lambda e, r=r: e.tensor_copy(out=lo_bf[r:r + 1, :], in_=xin[1][r:r + 1, 1024:2048]), reads=[("xin", 1)], writes=[("lo_bf",)])
            dma("sp", bsHL[r + 1:r + 2, :], lo_bf[r:r + 1, :], [("lo_bf",)], [("bsHL",)], "c_bslo")
        dma("sp", gmixT[:], gmixT_d[:, :], [], [("gmixT",)], "c_g1")
        dma("sp", gffnT[:], gffnT_d[:, :], [], [("gffnT",)], "c_g2")
        dma("sp", bgT[:], bgT_d[:, :], [], [("bgT",)], "c_bg")
        dma("sp", bcB[:], bcast_rows(lnb_d, D), [], [("bcB",)], "c_lnb")
        dma("sp", cs_all[:], cs_d.rearrange("(t p) c -> p t c", p=128), [], [("cs",)], "c_cs")

        slot_ctr = [0]

        part_ctr = [0]
        cached = {}
        cur_pass = [0]
        wide = [False]
        wide_ctr = [0]

        def wkey(slot):
            return ("w", slot) if slot < 4 else ("R1", "w", slot)

        def load_part(w, r0, n, c0, cache=True):
            if wide[0]:
                slot = wide_ctr[0] % 8
                wide_ctr[0] += 1
            else:
                slot = slot_ctr[0] % 4
                slot_ctr[0] += 1
            pid = None
            if cache:
                pid = part_ctr[0]
                part_ctr[0] += 1
            if pid is not None and cached.get(pid):
                src = wsc[pid][:, 0:n * 512].rearrange("p (k n) -> p k n", k=n)
                dma("pool", wslot[slot][:, 0:n, :], src, [("wsc", pid)], [wkey(slot)], "w%d" % slot)
                return slot, None
            if pid is None:
                assert (w, c0, n) == ("w_in", 6144, 8) and r0 in (0, 1024)
                pid_src = r0 // 1024
            else:
                pid_src = pid
                if cur_pass[0] == 0:
                    PART_SPECS.append((w, r0, n, c0))
                else:
                    assert PART_SPECS[pid] == (w, r0, n, c0)
            src = wpk[pid_src][:, 0:n * 512].rearrange("p (k n) -> p k n", k=n)
            dma("pool", wslot[slot][:, 0:n, :], src, [], [wkey(slot)], "w%d" % slot)
            store = pid is not None and (cur_pass[0] == 1 and pid % 5 == 0)
            if not store:
                return slot, None

            def wb(slot=slot, pid=pid, n=n):
                dma("act", wsc[pid][:, 0:n * 512].rearrange("p (k n) -> p k n", k=n), wslot[slot][:, 0:n, :],
                    [wkey(slot)], [("wsc", pid)], "wb%d" % slot)
                cached[pid] = True
            return slot, wb

        def kparts(nk):
            out, k0 = [], 0
            while k0 < nk:
                n = min(8, nk - k0)
                out.append((k0, n))
                k0 += n
            return out

        def mm_B(w, r0, nk, c0, rhsT, rhs_keys):
            banks = [nb() for _ in range(4)]
            parts = kparts(nk)
            for pi, (k0, n) in enumerate(parts):
                slot, wb = load_part(w, r0 + k0 * 128, n, c0)
                for ct in range(4):
                    def fn(e, slot=slot, ct=ct, k0=k0, n=n, pi=pi):
                        ins = None
                        for kc in range(n):
                            ins = e.matmul(ps[:, banks[ct], 0:T], wslot[slot][:, kc, ct * 128:(ct + 1) * 128], rhsT[:, k0 + kc, :],
                                           start=(pi == 0 and kc == 0), stop=(pi == len(parts) - 1 and kc == n - 1))
                        return ins
                    P.add("pe", fn, reads=[wkey(slot)] + rhs_keys, writes=[("ps", banks[ct])])
                if wb is not None:
                    wb()
            return banks

        def mm_A(w, r0, nk, c0, lhs_list, cache=True):
            banks = [nb() for _ in lhs_list]
            parts = kparts(nk)
            for pi, (k0, n) in enumerate(parts):
                slot, wb = load_part(w, r0 + k0 * 128, n, c0, cache)
                for ti, (lhsT, tok0, keys) in enumerate(lhs_list):
                    def fn(e, slot=slot, ti=ti, lhsT=lhsT, tok0=tok0, k0=k0, n=n, pi=pi):
                        ins = None
                        for kc in range(n):
                            ins = e.matmul(ps[:, banks[ti], :], lhsT[:, k0 + kc, tok0:tok0 + 128], wslot[slot][:, kc, :],
                                           start=(pi == 0 and kc == 0), stop=(pi == len(parts) - 1 and kc == n - 1))
                        return ins
                    P.add("pe", fn, reads=[wkey(slot)] + keys, writes=[("ps", banks[ti])])
                if wb is not None:
                    wb()
            return banks

        def norm_T(src_ap, src_keys, gT, gkey, dstT, tok0, dst_keys):
            norm_part(src_ap, src_keys)
            transp_part(gT, gkey, dstT, tok0, dst_keys)

        def norm_part(src_ap, src_keys):
            P.add("act", lambda e: e.activation(out=xn_tm, in_=src_ap, func=AF.Square, accum_out=stat[:, 0:1]),
                  reads=src_keys, writes=[("SC",), ("stat", 0)])
            P.add("act", lambda e: e.activation(out=stat[:, 1:2], in_=stat[:, 0:1], func=AF.Sqrt, scale=1.0 / D, bias=EPS),
                  reads=[("stat", 0)], writes=[("stat", 1)])
            P.add("dve", lambda e: e.reciprocal(out=stat[:, 2:3], in_=stat[:, 1:2]), reads=[("stat", 1)], writes=[("stat", 2)])
            P.add("dve", lambda e: e.tensor_scalar(out=xn_tm, in0=src_ap, scalar1=stat[:, 2:3], scalar2=None, op0=ALU.mult),
                  reads=src_keys + [("stat", 2)], writes=[("SC",)])

        def transp_part(gT, gkey, dstT, tok0, dst_keys):
            for h in range(2):
                tb = ntb()

                def fn(e, h=h, tb=tb):
                    ins = None
                    for c in range(8):
                        ins = e.transpose(pst[:, tb, c * 128:(c + 1) * 128], xn_tm[:, (h * 8 + c) * 128:(h * 8 + c + 1) * 128], ident[:])
                    return ins
                P.add("pe", fn, reads=[("SC",), ("ident",)], writes=[("pst", tb)])
                gb = gT[:, h * 8:(h + 1) * 8]
                g_bc = bass.AP(gb.tensor, gb.offset, [list(gb.ap[0]), [1, 8], [0, 128]])
                P.add("dve", lambda e, h=h, tb=tb, g_bc=g_bc: e.tensor_tensor(
                    out=dstT[:, h * 8:(h + 1) * 8, tok0:tok0 + 128], in0=pst[:, tb, :].rearrange("p (c t) -> p c t", c=8),
                    in1=g_bc, op=ALU.mult), reads=[("pst", tb), gkey], writes=dst_keys)

        def rope(src3, dst3, ct, nh, extra_reads, dst_keys):
            cb_ = cs_all[:, ct, 0:8]
            sb_ = cs_all[:, ct, 8:16]
            cos = bass.AP(cb_.tensor, cb_.offset, [list(cb_.ap[0]), [0, nh], [1, 8]])
            sin = bass.AP(sb_.tensor, sb_.offset, [list(sb_.ap[0]), [0, nh], [1, 8]])
            a, b_ = rt1[:, 0:nh, :], rt2[:, 0:nh, :]
            x1_, x2_ = src3[:, :, 0:8], src3[:, :, 8:16]
            rd = extra_reads + [("cs",)]
            P.add("dve", lambda e: e.tensor_tensor(out=a, in0=x1_, in1=cos, op=ALU.mult), reads=rd, writes=[("rt1",)])
            P.add("dve", lambda e: e.tensor_tensor(out=b_, in0=x2_, in1=sin, op=ALU.mult), reads=rd, writes=[("rt2",)])
            P.add("dve", lambda e: e.tensor_tensor(out=dst3[:, :, 0:8], in0=a, in1=b_, op=ALU.subtract),
                  reads=[("rt1",), ("rt2",)], writes=dst_keys)
            P.add("dve", lambda e: e.tensor_tensor(out=a, in0=x2_, in1=cos, op=ALU.mult), reads=rd, writes=[("rt1",)])
            P.add("dve", lambda e: e.tensor_tensor(out=b_, in0=x1_, in1=sin, op=ALU.mult), reads=rd, writes=[("rt2",)])
            P.add("dve", lambda e: e.tensor_tensor(out=dst3[:, :, 8:16], in0=a, in1=b_, op=ALU.add),
                  reads=[("rt1",), ("rt2",)], writes=dst_keys)

        def kv_tile(b, slot_kv, ct, halo, out_k=None, out_v=None, ki=0):
            psk = ps[:, b, 0:256].rearrange("p (h d) -> p h d", h=4)
            P.add("act", lambda e: e.activation(out=k32[:], in_=psk, func=AF.Copy), reads=[("ps", b)], writes=[("k32",)])
            P.add("act", lambda e: e.activation(out=va32[:], in_=ps[:, b, 256:512], func=AF.Copy), reads=[("ps", b)], writes=[("va32",)])
            rope(psk, k32, ct, 4, [("ps", b)], [("k32",)])
            if halo:
                P.add("dve", lambda e: e.tensor_scalar(out=Vb[:, slot_kv, :], in0=ps[:, b, 256:512], scalar1=hv_sb[:, 0:1], scalar2=None, op0=ALU.mult),
                      reads=[("ps", b), ("hv",)], writes=[("V", slot_kv)])
            else:
                P.add("dve", lambda e: e.tensor_copy(out=Vb[:, slot_kv, :], in_=ps[:, b, 256:512]), reads=[("ps", b)], writes=[("V", slot_kv)])
            if out_k is not None:
                dma("sp", out_k[:, :], k32[:].rearrange("p h d -> p (h d)"), [("k32",)], [("o_k",)], "o_k")
                dma("sp", out_v[:, :], va32[:], [("va32",)], [("o_v",)], "o_v")
            k_dup(ki)

        def k_dup(ki):
            kb = kbf2s[ki]
            P.add("act", lambda e: e.activation(out=kb[:, :, 0, :], in_=k32[:], func=AF.Copy), reads=[("k32",)], writes=[("kbf2", ki)])
            P.add("dve", lambda e: e.tensor_copy(out=kb[:, :, 1, :], in_=k32[:]), reads=[("k32",)], writes=[("kbf2", ki)])

        def kt_transpose(slot_kv, ki=0):
            kb = kbf2s[ki]
            tb = ntb()

            def fn(e):
                ins = None
                for h in range(4):
                    ins = e.transpose(pst[:, tb, h * 128:(h + 1) * 128], kb[:, h, :, :].rearrange("p a d -> p (a d)"), ident[:])
                return ins
            P.add("pe", fn, reads=[("kbf2", ki), ("ident",)], writes=[("pst", tb)])
            for h in range(4):
                P.add("act", lambda e, h=h: e.activation(out=KT2[:, h, slot_kv * 128:(slot_kv + 1) * 128],
                                                         in_=pst[:, tb, h * 128:(h + 1) * 128], func=AF.Copy),
                      reads=[("pst", tb)], writes=[("KT", slot_kv)])

        att_ctr = [0]

        att_units = []

        def attention_chunk(s, qh, full_slot, half_slot, half_hi, use_hv_full, use_hv_half):
            tq0 = (s - 1) * 128 + qh * 64
            hp = slice(64, 128) if half_hi else slice(0, 64)

            def one(kvh):
                st = {}

                def stageS():
                    st["i"] = i = att_ctr[0] % 2
                    att_ctr[0] += 1
                    st["bS"] = bS = [nb(), nb()]
                    for par in range(2):
                        pp = slice(par * 64, par * 64 + 64)

                        def fnS(e, par=par, pp=pp):
                            ins = None
                            for kt, slot in ((0, full_slot), (1, half_slot)):
                                ins = e.matmul(ps[:, bS[par], kt * 256:(kt + 1) * 256],
                                               KT2[pp, kvh, slot * 128:(slot + 1) * 128],
                                               QT2[pp, kvh * 4:(kvh + 1) * 4, tq0:tq0 + 64],
                                               start=True, stop=True)
                            return ins
                        P.add("pe", fnS, reads=[("KT", full_slot), ("KT", half_slot), ("R56", "q", s)], writes=[("ps", bS[par])])
                        P.add("act", lambda e, par=par: e.activation(
                            out=PT[i][:, par, :, :], in_=ps[:, bS[par], :].rearrange("p (k n) -> p k n", k=2), func=AF.Exp, scale=0.125),
                            reads=[("ps", bS[par])], writes=[("SC", "pt", i, par)])

                def stagePV():
                    i = st["i"]
                    bO = nb()

                    def fnO(e):
                        ins = None
                        for par in range(2):
                            pp = slice(par * 64, par * 64 + 64)
                            e.matmul(ps[pp, bO, 0:256], Vb[:, full_slot, kvh * 64:(kvh + 1) * 64], PT[i][:, par, 0, :],
                                     start=True, stop=False, skip_group_check=True)
                            e.matmul(ps[pp, bO, 0:256], Vb[hp, half_slot, kvh * 64:(kvh + 1) * 64], PT[i][hp, par, 1, :],
                                     start=False, stop=True, skip_group_check=True)
                        for par in range(2):
                            pp = slice(par * 64, par * 64 + 64)
                            of = onesH_bf if use_hv_full else ones_bf
                            oh = onesH_bf if use_hv_half else ones_bf
                            e.matmul(ps[pp, bO, 256:512], of[:, 0:64], PT[i][:, par, 0, :], start=True, stop=False, skip_group_check=True)
                            e.matmul(ps[pp, bO, 256:512], oh[hp, 0:64], PT[i][hp, par, 1, :], start=False, stop=False, skip_group_check=True)
                        ins = e.matmul(ps[:, bO, 256:512], esHL[0:8, kvh * 128:(kvh + 1) * 128], sel8[0:8, :], start=False, stop=True,
                                       skip_group_check=True)
                        return ins
                    P.add("pe", fnO, reads=[("V", full_slot), ("V", half_slot), ("SC", "pt", i, 0), ("SC", "pt", i, 1),
                                            ("ones_bf",), ("onesH",), ("esHL",), ("sel8",)], writes=[("ps", bO)])
                    P.add("dve", lambda e: e.reciprocal(out=rden[i], in_=ps[:, bO, 256:512]), reads=[("ps", bO)], writes=[("SC", "rd", i)])
                    P.add("dve", lambda e: e.tensor_tensor(out=oT[:, kvh * 4:(kvh + 1) * 4, tq0:tq0 + 64],
                                                           in0=ps[:, bO, 0:256].rearrange("p (j q) -> p j q", j=4),
                                                           in1=rden[i].rearrange("p (j q) -> p j q", j=4), op=ALU.mult),
                          reads=[("ps", bO), ("SC", "rd", i)], writes=[("R56", "o", s)])
                att_units.append((stageS, stagePV))
            for kvh in range(4):
                one(kvh)

        def attention_flush():
            n = len(att_units)
            if n:
                att_units[0][0]()
            for u in range(n):
                if u + 1 < n:
                    att_units[u + 1][0]()
                att_units[u][1]()
            del att_units[:]

        xin_ctr = [0]
        cs_ctr = [0]
        for p in range(NPASS):
            cur_pass[0] = p
            part_ctr[0] = 0
            tiles = []
            for s in range(1, NT + 1):
                row0 = p * T + (s - 1) * 128
                tiles.append((s, row0, (p == 2 and s == 3)))

            if p == 0:
                P.relayout("SC")
                P.relayout("R3")
                xb = xin_ctr[0] % 2
                xin_ctr[0] += 1
                dma("sp", xin[xb][:], xhalo[:, :], [], [("xin", xb)], "xin%d" % xb)
                norm_T(xin[xb][:], [("xin", xb)], gmixT, ("gmixT",), xnTh, 0, [("R3", "h")])
                for (s, row0, samp) in tiles:
                    xb = xin_ctr[0] % 2
                    xin_ctr[0] += 1
                    dma("sp", xin[xb][:], xtok[row0:row0 + 128, :], [], [("xin", xb)], "xin%d" % xb)
                    norm_T(xin[xb][:], [("xin", xb)], gmixT, ("gmixT",), xnT, (s - 1) * 128, [("R2", s)])
            xkeys = [("R2", s) for s in range(1, NT + 1)]
            dma("sp", bcA[:], bcast_rows(lng_d, D), [], [("bcA",)], "bcA")

            P.relayout("R1")
            P.relayout("R4")
            P.relayout("R56")
            tl = [(xnT, (s - 1) * 128, [("R2", s)]) for (s, row0, samp) in tiles]
            if p == 0:
                hb = mm_A(w_in, 0, 16, 6144, [(xnTh, 0, [("R3", "h")])], cache=False)
                kv_tile(hb[0], 0, 0, True)
                kt_transpose(0)
            if p == 2:
                for q_ in range(2):
                    dma("sp", k32[:].rearrange("p h d -> p (h d)"), ck_d[q_ * 128:(q_ + 1) * 128, :], [], [("k32",)], "ck")
                    k_dup(q_)
                    kt_transpose(4 + q_, q_)
                    dma("sp", va32[:], cv_d[q_ * 128:(q_ + 1) * 128, :], [], [("va32",)], "cv")
                    P.add("dve", lambda e, q_=q_: e.tensor_copy(out=Vb[:, 4 + q_, :], in_=va32[:]), reads=[("va32",)], writes=[("V", 4 + q_)])
            kvbanks = mm_A(w_in, 0, 16, 6144, tl)

            def post_kv(kvbanks=kvbanks, p=p, tiles=tiles):
                for ti, (s, row0, samp) in enumerate(tiles):
                    ok = ov = None
                    if p == 2 and s == 2:
                        ok, ov = kp_o, vp_o
                    if samp:
                        ok, ov = ks_o, vs_o
                    kv_tile(kvbanks[ti], s, 1 + row0 // 128, False, ok, ov, ti)
                for ti, (s, row0, samp) in enumerate(tiles):
                    kt_transpose(s, ti)

            def post_q(h4, qbanks, tiles=tiles):
                for ti, (s, row0, samp) in enumerate(tiles):
                    b = qbanks[ti]
                    qt = q_tm[ti]
                    P.add("act", lambda e, qt=qt, b=b: e.activation(out=qt[:], in_=ps[:, b, :], func=AF.Copy),
                          reads=[("ps", b)], writes=[("qtm", ti)])
                    rope(ps[:, b, :].rearrange("p (h d) -> p h d", h=8), qt[:].rearrange("p (h d) -> p h d", h=8),
                         1 + row0 // 128, 8, [("ps", b)], [("qtm", ti)])
                for ti, (s, row0, samp) in enumerate(tiles):
                    qt = q_tm[ti]
                    tb = ntb()

                    def fnq(e, qt=qt, tb=tb):
                        ins = None
                        for j in range(4):
                            ins = e.transpose(pst[:, tb, j * 128:(j + 1) * 128], qt[:, j * 128:(j + 1) * 128], ident[:])
                        return ins
                    P.add("pe", fnq, reads=[("qtm", ti), ("ident",)], writes=[("pst", tb)])
                    P.add("dve", lambda e, tb=tb, h4=h4, s=s: e.tensor_copy(
                        out=QT2[:, h4 * 4:(h4 + 1) * 4, (s - 1) * 128:s * 128],
                        in_=pst[:, tb, 0:512].rearrange("p (j t) -> p j t", j=4)),
                        reads=[("pst", tb)], writes=[("R56", "q", s)])

            pending = post_kv
            for h4 in range(4):
                qbanks = mm_A(w_in, 0, 16, 4096 + h4 * 512, tl)
                pending()
                pending = (lambda h4=h4, qbanks=qbanks: post_q(h4, qbanks))
            for c4 in range(4):
                vbanks = mm_A(w_in, 0, 16, 2048 + c4 * 512, tl)
                if pending is not None:
                    pending()
                    pending = None
                for ti, (s, row0, samp) in enumerate(tiles):
                    b = vbanks[ti]
                    P.add("act", lambda e, b=b, s=s, c4=c4: e.activation(out=v_g[:, s - 1, c4 * 512:(c4 + 1) * 512], in_=ps[:, b, :],
                                                                          func=AF.Gelu_apprx_tanh),
                          reads=[("ps", b)], writes=[("R1", "v", s)])
            def layernorm_tile(s, samp):
                vt = v_g[:, s - 1, :]
                for c4 in range(4):
                    P.add("dve", lambda e, c4=c4, vt=vt: e.bn_stats(out=stat[:, 8 + c4 * 6:14 + c4 * 6], in_=vt[:, c4 * 512:(c4 + 1) * 512]),
                          reads=[("R1", "v", s)], writes=[("stat", "bn", c4)])
                P.add("dve", lambda e: e.bn_aggr(out=stat[:, 32:34], in_=stat[:, 8:32].rearrange("p (c k) -> p c k", c=4)),
                      reads=[("stat", "bn", c4) for c4 in range(4)], writes=[("stat", "mv")])
                P.add("act", lambda e: e.activation(out=stat[:, 34:35], in_=stat[:, 33:34], func=AF.Sqrt, scale=1.0, bias=EPS),
                      reads=[("stat", "mv")], writes=[("stat", "sd")])
                P.add("dve", lambda e: e.reciprocal(out=stat[:, 35:36], in_=stat[:, 34:35]), reads=[("stat", "sd")], writes=[("stat", "rs")])
                P.add("dve", lambda e, vt=vt: e.tensor_scalar(out=vt, in0=vt, scalar1=stat[:, 32:33], scalar2=stat[:, 35:36],
                                                               op0=ALU.subtract, op1=ALU.mult),
                      reads=[("R1", "v", s), ("stat", "mv"), ("stat", "rs")], writes=[("R1", "v", s)])
                P.add("dve", lambda e, vt=vt: e.tensor_tensor(out=vt, in0=vt, in1=bcA[:], op=ALU.mult),
                      reads=[("R1", "v", s), ("bcA",)], writes=[("R1", "v", s)])
                if samp:
                    P.add("dve", lambda e, vt=vt: e.tensor_tensor(out=vt, in0=vt, in1=bcB[:], op=ALU.add),
                          reads=[("R1", "v", s), ("bcB",)], writes=[("R1", "v", s)])
                    P.add("dve", lambda e, vt=vt, s=s: e.tensor_copy(out=vn_bf[:, s - 1, :], in_=vt),
                          reads=[("R1", "v", s)], writes=[("R4", "vn", s)])
                    dma("sp", gv_o[:, :], vt, [("R1", "v", s)], [("o_gv",)], "o_gv")
                else:
                    P.add("dve", lambda e, vt=vt, s=s: e.tensor_tensor(out=vn_bf[:, s - 1, :], in0=vt, in1=bcB[:], op=ALU.add),
                          reads=[("R1", "v", s), ("bcB",)], writes=[("R4", "vn", s)])

            P.relayout("R3")
            for c4 in range(4):
                ubanks = mm_B(w_in, 0, 16, c4 * 512, xnT, xkeys)
                for ct in range(4):
                    b = ubanks[ct]
                    c = c4 * 4 + ct
                    P.add("act", lambda e, b=b, c=c: e.activation(out=uT[:, c, :], in_=ps[:, b, 0:T], func=AF.Gelu_apprx_tanh),
                          reads=[("ps", b)], writes=[("R3", c)])
                if c4 < NT:
                    (s_, row0_, samp_) = tiles[c4]
                    layernorm_tile(s_, samp_)

            P.relayout("SC")
            for (s, row0, samp) in tiles:
                wm = wmS if samp else wmP
                wmk = ("wmS",) if samp else ("wmP",)
                brow = 32 if samp else 0
                tk0 = (s - 1) * 128
                for c4 in range(4):
                    b = nb()

                    def fnm(e, b=b, c4=c4, s=s, wm=wm, brow=brow):
                        ins = None
                        for ct in range(4):
                            c = c4 * 4 + ct
                            g = c // 2
                            e.matmul(ps[:, b, ct * 128:(ct + 1) * 128], vn_bf[:, s - 1, c * 128:(c + 1) * 128], wm[:, g, :],
                                     start=True, stop=False, skip_group_check=True)
                            ins = e.matmul(ps[:, b, ct * 128:(ct + 1) * 128], ones_bf[brow:brow + 2, :], bsHL[brow:brow + 2, g * 128:(g + 1) * 128],
                                           start=False, stop=True, skip_group_check=True)
                        return ins
                    P.add("pe", fnm, reads=[("R4", "vn", s), wmk, ("ones_bf",), ("bsHL",)], writes=[("ps", b)])
                    ukeys = [("R3", c4 * 4 + ct) for ct in range(4)]
                    P.add("dve", lambda e, b=b, c4=c4, tk0=tk0: e.tensor_tensor(
                        out=uT[:, c4 * 4:(c4 + 1) * 4, tk0:tk0 + 128], in0=ps[:, b, :].rearrange("p (c t) -> p c t", c=4),
                        in1=uT[:, c4 * 4:(c4 + 1) * 4, tk0:tk0 + 128], op=ALU.mult),
                        reads=[("ps", b)] + ukeys, writes=ukeys)
            for (s, row0, samp) in tiles:
                if samp:
                    attention_chunk(s, 0, 4, s, False, False, False)
                    attention_chunk(s, 1, 5, s, True, False, False)
                else:
                    h0 = (p == 0 and s == 1)
                    attention_chunk(s, 0, s - 1, s, False, h0, False)
                    attention_chunk(s, 1, s, s - 1, True, False, h0)
            attention_flush()
            if p < 2:
                P.add("act", lambda e: e.activation(out=KT2[:, :, 0:128], in_=KT2[:, :, 3 * 128:4 * 128], func=AF.Copy),
                      reads=[("KT", 3)], writes=[("KT", 0)])
                P.add("dve", lambda e: e.tensor_copy(out=Vb[:, 0, :], in_=Vb[:, 3, :]), reads=[("V", 3)], writes=[("V", 0)])

            P.relayout("SC")
            P.relayout("R4")
            P.relayout("R56")
            P.relayout("R1")
            wide[0] = True
            wide_ctr[0] = 0
            akeys = [("R3", c) for c in range(16)]
            okeys = [("R56", "o", s) for s in range(1, NT + 1)]
            for j4 in range(4):
                dbanks = mm_B(w_gate, 0, 16, j4 * 512, xnT, xkeys)
                for ct in range(4):
                    b = dbanks[ct]
                    P.add("act", lambda e, b=b, ct=ct, j4=j4: e.activation(out=gA[:, ct, :], in_=ps[:, b, 0:T], func=AF.Sigmoid,
                                                                            bias=bgT[:, j4 * 4 + ct:j4 * 4 + ct + 1], scale=1.0),
                          reads=[("ps", b), ("bgT",)], writes=[("R56", "gA", ct)])
                dbanks = mm_B(w_a, 0, 16, j4 * 512, uT, akeys)
                for ct in range(4):
                    b = dbanks[ct]
                    P.add("dve", lambda e, b=b, ct=ct: e.tensor_tensor(out=tA[:, ct, :], in0=ps[:, b, 0:T], in1=gA[:, ct, :], op=ALU.mult),
                          reads=[("ps", b), ("R56", "gA", ct)], writes=[("R56", "tA", ct)])
                dbanks = mm_B(w_gate, 0, 16, 2048 + j4 * 512, xnT, xkeys)
                for ct in range(4):
                    b = dbanks[ct]
                    P.add("act", lambda e, b=b, ct=ct, j4=j4: e.activation(out=gB[:, ct, :], in_=ps[:, b, 0:T], func=AF.Sigmoid,
                                                                            bias=bgT[:, 16 + j4 * 4 + ct:16 + j4 * 4 + ct + 1], scale=1.0),
                          reads=[("ps", b), ("bgT",)], writes=[("R56", "gB", ct)])
                dbanks = mm_B(w_b, 0, 16, j4 * 512, oT, okeys)
                for ct in range(4):
                    b = dbanks[ct]
                    ti = ct % 2
                    P.add("dve", lambda e, b=b, ct=ct, ti=ti: e.tensor_tensor(out=t2[ti], in0=ps[:, b, 0:T], in1=gB[:, ct, :], op=ALU.mult),
                          reads=[("ps", b), ("R56", "gB", ct)], writes=[("SC", "t2", ti)])
                    P.add("dve", lambda e, ct=ct, ti=ti, j4=j4: e.tensor_tensor(out=mT[:, j4 * 4 + ct, :], in0=t2[ti], in1=tA[:, ct, :], op=ALU.add),
                          reads=[("SC", "t2", ti), ("R56", "tA", ct)], writes=[("R4", "m", j4 * 4 + ct)])

            P.relayout("R56")
            mkeys = [("R4", "m", c) for c in range(16)]
            P.relayout("xin")
            xres_list = {}

            def issue_xres(i, tiles=tiles):
                c4_, ti_ = i // NT, i % NT
                row0_ = tiles[ti_][1]
                xs = i % 8
                xres_list[i] = xs
                dma("sp", xin[xs // 4][:, (xs % 4) * 512:(xs % 4 + 1) * 512], xtok[row0_:row0_ + 128, c4_ * 512:(c4_ + 1) * 512],
                    [], [("xin", "r", xs)], "xr%d" % xs)
            for i_ in range(8):
                issue_xres(i_)
            for c4 in range(4):
                ebanks = mm_A(w_o, 0, 16, c4 * 512, [(mT, (s - 1) * 128, mkeys) for (s, row0, samp) in tiles])
                for ti, (s, row0, samp) in enumerate(tiles):
                    b = ebanks[ti]
                    xs = xres_list[c4 * NT + ti]
                    P.add("dve", lambda e, b=b, s=s, c4=c4, xs=xs: e.tensor_tensor(out=x1[:, s - 1, c4 * 512:(c4 + 1) * 512], in0=ps[:, b, :],
                                                                                    in1=xin[xs // 4][:, (xs % 4) * 512:(xs % 4 + 1) * 512], op=ALU.add),
                          reads=[("ps", b), ("xin", "r", xs)], writes=[("R56", "x1", s)])
                    nxt_i = c4 * NT + ti + 8
                    if nxt_i < 4 * NT:
                        issue_xres(nxt_i)

            wide[0] = False
            P.relayout("SC")
            P.relayout("xin")
            for (s, row0, samp) in tiles:
                norm_T(x1[:, s - 1, :], [("R56", "x1", s)], gffnT, ("gffnT",), xnT, (s - 1) * 128, [("R2", s)])
            dma("sp", bcA[:], bcast_rows(fing_d, D), [], [("bcA",)], "bcA")

            P.relayout("R1")
            P.relayout("R3")
            sgv = R3[:, 0:8, :]
            for f4 in range(11):
                sgi = f4 % 2
                gbanks = mm_B(w_fg, 0, 16, f4 * 512, xnT, xkeys)
                for ct in range(4):
                    b = gbanks[ct]
                    P.add("act", lambda e, b=b, ct=ct, sgi=sgi: e.activation(out=sgv[:, sgi * 4 + ct, :], in_=ps[:, b, 0:T], func=AF.Silu),
                          reads=[("ps", b)], writes=[("R3", "sg", sgi, ct)])
                gbanks = mm_B(w_fu, 0, 16, f4 * 512, xnT, xkeys)
                for ct in range(4):
                    b = gbanks[ct]
                    P.add("dve", lambda e, b=b, ct=ct, sgi=sgi, f4=f4: e.tensor_tensor(out=hid[:, f4 * 4 + ct, :], in0=ps[:, b, 0:T],
                                                                                        in1=sgv[:, sgi * 4 + ct, :], op=ALU.mult),
                          reads=[("ps", b), ("R3", "sg", sgi, ct)], writes=[("R1", "h", f4)])

            hkeys = [("R1", "h", f4) for f4 in range(11)]
            P.relayout("SC")
            for c4 in range(4):
                nxt = (p + 1 < NPASS and c4 < NT)
                if nxt:
                    nrow0 = (p + 1) * T + c4 * 128
                    xb = xin_ctr[0] % 2
                    xin_ctr[0] += 1
                    dma("sp", xin[xb][:], xtok[nrow0:nrow0 + 128, :], [], [("xin", xb)], "xin%d" % xb)
                    norm_part(xin[xb][:], [("xin", xb)])
                banks = mm_A(w_fd, 0, 44, c4 * 512, [(hid, (s - 1) * 128, hkeys) for (s, row0, samp) in tiles])
                if nxt:
                    transp_part(gmixT, ("gmixT",), xnT, c4 * 128, [("R2", c4 + 1)])
                for ti, (s, row0, samp) in enumerate(tiles):
                    b = banks[ti]
                    P.add("dve", lambda e, b=b, s=s, c4=c4: e.tensor_tensor(out=x1[:, s - 1, c4 * 512:(c4 + 1) * 512], in0=ps[:, b, :],
                                                                             in1=x1[:, s - 1, c4 * 512:(c4 + 1) * 512], op=ALU.add),
                          reads=[("ps", b), ("R56", "x1", s)], writes=[("R56", "x1", s)])
            for (s, row0, samp) in tiles:
                xt = x1[:, s - 1, :]
                xb = xin_ctr[0] % 2
                xin_ctr[0] += 1
                P.add("act", lambda e, xt=xt: e.activation(out=xn_tm, in_=xt, func=AF.Square, accum_out=stat[:, 40:41]),
                      reads=[("R56", "x1", s)], writes=[("SC",), ("stat", 40)])
                P.add("act", lambda e: e.activation(out=stat[:, 41:42], in_=stat[:, 40:41], func=AF.Sqrt, scale=1.0 / D, bias=EPS),
                      reads=[("stat", 40)], writes=[("stat", 41)])
                P.add("dve", lambda e: e.reciprocal(out=stat[:, 42:43], in_=stat[:, 41:42]), reads=[("stat", 41)], writes=[("stat", 42)])
                P.add("dve", lambda e, xt=xt, xb=xb: e.scalar_tensor_tensor(out=xin[xb][:], in0=xt, scalar=stat[:, 42:43], in1=bcA[:],
                                                                             op0=ALU.mult, op1=ALU.mult),
                      reads=[("R56", "x1", s), ("stat", 42), ("bcA",)], writes=[("xin", xb)])
                dma("sp", y_o[row0:row0 + 128, :], xin[xb][:], [("xin", xb)], [("o_y", xb)], "oy%d" % xb)

        P.emit(nc, es)
    return nc


def _rope_tables(pos):
    half = 8
    inv_freq = np.float32(500000.0) ** (-(np.arange(half, dtype=np.float32) * np.float32(2.0) / np.float32(16)))
    ang = pos.astype(np.float32)[:, None] * inv_freq[None, :].astype(np.float32)
    cos = np.cos(ang).astype(np.float32)
    sin = np.sin(ang).astype(np.float32)
    cs = np.concatenate([cos, sin], axis=1)
    return np.ascontiguousarray(cs, dtype=np.float32)


_NC_CACHE = {}


def _prepare(x_prompt, x_sample, cache_swa_k, cache_swa_v, norm_mix_g, w_in, gmlp_ln_g, gmlp_ln_b,
           gmlp_ws, gmlp_bs, attn_sinks, w_gate, b_gate, w_branch_a, w_branch_b, w_out,
           norm_ffn_g, w_ffn_gate, w_ffn_up, w_ffn_down, final_norm_g):
    f = lambda a: np.ascontiguousarray(np.asarray(a), dtype=np.float32)
    x_prompt, x_sample = f(x_prompt), f(x_sample)
    ck_all, cv_all = f(cache_swa_k)[0], f(cache_swa_v)[0]
    _get_nc()
    wsrc = {"w_in": f(w_in)[0], "w_gate": f(w_gate)[0], "w_a": f(w_branch_a)[0], "w_b": f(w_branch_b)[0],
            "w_o": f(w_out)[0], "w_fg": f(w_ffn_gate)[0], "w_fu": f(w_ffn_up)[0], "w_fd": f(w_ffn_down)[0]}
    assert len(PART_SPECS) == NPARTS and PART_SPECS[0] == ("w_in", 0, 8, 6144) and PART_SPECS[1] == ("w_in", 1024, 8, 6144)
    wpk = np.zeros((NPARTS, 128, 4096), np.float32)
    for pid, (wn, r0, n, c0) in enumerate(PART_SPECS):
        blk = wsrc[wn][r0:r0 + n * 128, c0:c0 + 512].reshape(n, 128, 512).transpose(1, 0, 2)
        wpk[pid, :, :n * 512] = blk.reshape(128, n * 512)
    shared = {
        "wpk": wpk,
        "gmixT": np.ascontiguousarray(f(norm_mix_g)[0].reshape(16, 128).T),
        "gffnT": np.ascontiguousarray(f(norm_ffn_g)[0].reshape(16, 128).T),
        "bgT": np.ascontiguousarray(f(b_gate)[0].reshape(32, 128).T),
        "lng": f(gmlp_ln_g)[0].reshape(1, D), "lnb": f(gmlp_ln_b)[0].reshape(1, D),
        "fing": f(final_norm_g).reshape(1, D),
        "wsT": np.ascontiguousarray(f(gmlp_ws)[0].transpose(2, 0, 1)),
        "bsP": f(gmlp_bs)[0].reshape(1, 1024),
        "bsS": np.ascontiguousarray(np.concatenate([f(gmlp_bs)[0][:, :64], f(gmlp_bs)[0][:, :64]], axis=1).reshape(1, 1024)),
        "ident": np.eye(128, dtype=np.float32),
    }
    sk = f(attn_sinks)[0]
    sinkE = np.zeros((4, 4, 128), np.float32)
    for jj in range(4):
        for kvh in range(4):
            for par in range(2):
                sinkE[jj, kvh, par * 64:(par + 1) * 64] = sk[kvh * 8 + 2 * jj + par]
    shared["sinkE"] = sinkE.reshape(4, 512)
    sel = np.zeros((4, 4, 64), np.float32)
    for jj in range(4):
        sel[jj, jj, :] = 1.0
    shared["sel"] = sel.reshape(4, 256)

    in_maps = []
    for c in range(8):
        b, half = c // 2, c % 2
        xt = np.concatenate([x_prompt[b, half * 1024:(half + 1) * 1024], x_sample[2 * c], x_sample[2 * c + 1]], axis=0)
        if half == 1:
            xh = x_prompt[b, 896:1024]
            pos_h = np.arange(896, 1024)
        else:
            xh = np.zeros((128, D), np.float32)
            pos_h = np.zeros(128)
        pos = np.concatenate([pos_h, half * 1024 + np.arange(1024), 2048 + np.arange(64), 2048 + np.arange(64)])
        m = dict(shared)
        m["xtok"] = np.ascontiguousarray(xt)
        m["xhalo"] = np.ascontiguousarray(xh)
        m["cs"] = _rope_tables(pos)
        m["hv"] = np.full((128, 64), float(half), np.float32)
        m["ck"] = np.ascontiguousarray(ck_all[2 * c:2 * c + 2].reshape(256, 256))
        m["cv"] = np.ascontiguousarray(cv_all[2 * c:2 * c + 2].reshape(256, 256))
        in_maps.append(m)

    return in_maps


def _get_nc():
    if "nc" not in _NC_CACHE:
        _NC_CACHE["nc"] = build_nc()
    return _NC_CACHE["nc"]


def kernel(**inputs):
    in_maps = _prepare(**inputs)
    nc = _get_nc()
    res = run_bass_kernel_spmd(nc, in_maps, core_ids=list(range(8)))
    return _assemble(res.results)


def _assemble(R):

    y_prompt = np.zeros((4, 2048, D), np.float32)
    y_sample = np.zeros((16, 64, D), np.float32)
    kp = np.zeros((1, 4, 128, 4, 64), np.float32)
    vp = np.zeros((1, 4, 128, 4, 64), np.float32)
    ks = np.zeros((1, 16, 64, 4, 64), np.float32)
    vs = np.zeros((1, 16, 64, 4, 64), np.float32)
    gv = np.zeros((1, 16, 64, D), np.float32)
    for c in range(8):
        b, half = c // 2, c % 2
        y = np.asarray(R[c]["y"])
        y_prompt[b, half * 1024:(half + 1) * 1024] = y[0:1024]
        y_sample[2 * c] = y[1024:1088]
        y_sample[2 * c + 1] = y[1088:1152]
        if half == 1:
            kp[0, b] = np.asarray(R[c]["kp"]).reshape(128, 4, 64)
            vp[0, b] = np.asarray(R[c]["vp"]).reshape(128, 4, 64)
        ksc = np.asarray(R[c]["ks"]).reshape(2, 64, 4, 64)
        vsc = np.asarray(R[c]["vs"]).reshape(2, 64, 4, 64)
        gvc = np.asarray(R[c]["gv"]).reshape(2, 64, D)
        ks[0, 2 * c:2 * c + 2] = ksc
        vs[0, 2 * c:2 * c + 2] = vsc
        gv[0, 2 * c:2 * c + 2] = gvc
    return (y_prompt, y_sample, kp, vp, ks, vs, gv)
```
